# Optimizing a Trainium2 kernel written in Bass

```python
import jax, jax.numpy as jnp
from jax import lax
import numpy as np

D_MODEL = 2048
BATCH = 2
SEQ = 4096
DEPTH = 1

D_MIX = D_MODEL
D_FOURIER = D_MIX // 2
N_FOURIER_GROUPS = 4
FOURIER_GROUP = D_FOURIER // N_FOURIER_GROUPS
D_MLSTM = D_MIX - D_FOURIER
N_MLSTM_HEADS = 4
HEAD_V = D_MLSTM // N_MLSTM_HEADS
HEAD_QK = HEAD_V // 2
D_QK = N_MLSTM_HEADS * HEAD_QK
N_GATES = 4 * N_MLSTM_HEADS
D_IN = D_FOURIER + 2 * D_QK + 2 * D_MLSTM + N_GATES
CHUNK = 64
D_FF = 5632
CONV_W = 3
N_MOD = 6
EPS = 1e-6
M_INIT = -1e30

kernel_name = "hybrid_fnet_mlstm_convffn_adaln_block"


def rms_norm(x, w):
    xf = x.astype(jnp.float32)
    y = xf * lax.rsqrt(jnp.mean(xf * xf, axis=-1, keepdims=True) + EPS)
    return (y * w.astype(jnp.float32)).astype(x.dtype)


def modulate(h, shift, scale):
    return h * (1 + scale[:, None, :]) + shift[:, None, :]


def fourier_mix(u):
    B, S, _ = u.shape
    ug = u.astype(jnp.float32).reshape(B, S, N_FOURIER_GROUPS, FOURIER_GROUP)
    y = jnp.fft.fft2(ug, axes=(1, 3), norm="ortho").real
    return y.reshape(B, S, D_FOURIER).astype(u.dtype)


def mlstm_chunkwise(q, k, v, ig, fg):
    B, H, S, Dk = q.shape
    Dv = v.shape[-1]
    NC = S // CHUNK
    lf = jax.nn.log_sigmoid(fg)
    q = q.reshape(B, H, NC, CHUNK, Dk) * (Dk ** -0.5)
    k = k.reshape(B, H, NC, CHUNK, Dk)
    v = v.reshape(B, H, NC, CHUNK, Dv)
    ig = ig.reshape(B, H, NC, CHUNK)
    lf = lf.reshape(B, H, NC, CHUNK)
    b = jnp.cumsum(lf, axis=-1)
    g = b[..., -1]

    a = g[..., None] - b + ig
    m_loc = jnp.max(a, axis=-1)
    w_loc = jnp.exp(a - m_loc[..., None])
    C_loc = jnp.einsum('bhclv,bhclk->bhcvk', v * w_loc[..., None], k)
    n_loc = jnp.einsum('bhcl,bhclk->bhck', w_loc, k)

    def step(carry, inp):
        C, n, m = carry
        g_c, m_c, C_c, n_c = inp
        m_new = jnp.maximum(g_c + m, m_c)
        s_old = jnp.exp(g_c + m - m_new)
        s_loc = jnp.exp(m_c - m_new)
        C_new = s_old[..., None, None] * C + s_loc[..., None, None] * C_c
        n_new = s_old[..., None] * n + s_loc[..., None] * n_c
        return (C_new, n_new, m_new), (C, n, m)

    init = (jnp.zeros((B, H, Dv, Dk), jnp.float32),
            jnp.zeros((B, H, Dk), jnp.float32),
            jnp.full((B, H), M_INIT, jnp.float32))
    xs = (jnp.moveaxis(g, 2, 0), jnp.moveaxis(m_loc, 2, 0),
          jnp.moveaxis(C_loc, 2, 0), jnp.moveaxis(n_loc, 2, 0))
    _, (C_st, n_st, m_st) = lax.scan(step, init, xs)
    C_st = jnp.moveaxis(C_st, 0, 2)
    n_st = jnp.moveaxis(n_st, 0, 2)
    m_st = jnp.moveaxis(m_st, 0, 2)

    tri = jnp.tril(jnp.ones((CHUNK, CHUNK), dtype=bool))
    Dlog = jnp.where(tri, b[..., :, None] - b[..., None, :] + ig[..., None, :], -jnp.inf)
    inter = b + m_st[..., None]
    m_row = jnp.maximum(jnp.max(Dlog, axis=-1), inter)
    P = jnp.exp(Dlog - m_row[..., None]) * jnp.einsum('bhcjd,bhctd->bhcjt', q, k)
    s_inter = jnp.exp(inter - m_row)
    num = (jnp.einsum('bhcjt,bhctv->bhcjv', P, v)
           + s_inter[..., None] * jnp.einsum('bhcjk,bhcvk->bhcjv', q, C_st))
    den = jnp.sum(P, axis=-1) + s_inter * jnp.einsum('bhcjk,bhck->bhcj', q, n_st)
    h = num / jnp.maximum(jnp.abs(den), jnp.exp(-m_row))[..., None]
    return h.reshape(B, H, S, Dv)


def mlstm_bidirectional(q_in, k_in, v_in, gates, mlstm_norm_w):
    B, S, _ = q_in.shape
    H = N_MLSTM_HEADS
    q = q_in.astype(jnp.float32).reshape(B, S, H, HEAD_QK).transpose(0, 2, 1, 3)
    k = k_in.astype(jnp.float32).reshape(B, S, H, HEAD_QK).transpose(0, 2, 1, 3)
    v = v_in.astype(jnp.float32).reshape(B, S, H, HEAD_V).transpose(0, 2, 1, 3)
    gt = gates.astype(jnp.float32).reshape(B, S, 4, H).transpose(2, 0, 3, 1)
    i_f, f_f, i_b, f_b = gt[0], gt[1], gt[2], gt[3]
    h_fwd = mlstm_chunkwise(q, k, v, i_f, f_f)
    h_bwd = jnp.flip(mlstm_chunkwise(jnp.flip(q, 2), jnp.flip(k, 2), jnp.flip(v, 2),
                                     jnp.flip(i_b, 2), jnp.flip(f_b, 2)), 2)
    h = (h_fwd + h_bwd).transpose(0, 2, 1, 3)
    h = h * lax.rsqrt(jnp.mean(h * h, axis=-1, keepdims=True) + EPS)
    h = h.reshape(B, S, D_MLSTM) * mlstm_norm_w.astype(jnp.float32)
    return h.astype(q_in.dtype)


def depthwise_conv_centred(u, w_conv, b_conv):
    C = u.shape[-1]
    y = lax.conv_general_dilated(u, w_conv[:, None, :].astype(u.dtype), window_strides=(1,),
                                 padding='SAME', dimension_numbers=('NWC', 'WIO', 'NWC'),
                                 feature_group_count=C)
    return y + b_conv


def setup_inputs(seed: int = 0) -> dict:
    key = jax.random.key(seed)
    ks = jax.random.split(key, 20)
    f32 = jnp.float32
    H = N_MLSTM_HEADS
    x = jax.random.normal(ks[0], (BATCH, SEQ, D_MODEL), f32)
    c = jax.random.normal(ks[1], (BATCH, D_MODEL), f32)
    w_ada = jax.random.normal(ks[2], (D_MODEL, N_MOD * D_MODEL), f32) * (0.5 * D_MODEL ** -0.5)
    b_ada = 0.02 * jax.random.normal(ks[3], (N_MOD * D_MODEL,), f32)
    norm1_w = 1.0 + 0.02 * jax.random.normal(ks[4], (D_MODEL,), f32)
    w_in = jax.random.normal(ks[5], (D_MODEL, D_IN), f32) * (D_MODEL ** -0.5)
    bi_f = -1.0 + 0.1 * jax.random.normal(ks[6], (H,), f32)
    bf_f = jnp.linspace(3.0, 6.0, H, dtype=f32) + 0.1 * jax.random.normal(ks[7], (H,), f32)
    bi_b = -1.0 + 0.1 * jax.random.normal(ks[8], (H,), f32)
    bf_b = jnp.linspace(3.0, 6.0, H, dtype=f32) + 0.1 * jax.random.normal(ks[9], (H,), f32)
    b_gates = jnp.concatenate([bi_f, bf_f, bi_b, bf_b])
    mlstm_norm_w = 1.0 + 0.02 * jax.random.normal(ks[10], (D_MLSTM,), f32)
    w_out = jax.random.normal(ks[11], (D_MIX, D_MODEL), f32) * (D_MIX ** -0.5)
    norm2_w = 1.0 + 0.02 * jax.random.normal(ks[12], (D_MODEL,), f32)
    w_up = jax.random.normal(ks[13], (D_MODEL, 2 * D_FF), f32) * (D_MODEL ** -0.5)
    w_conv = jax.random.normal(ks[14], (CONV_W, 2 * D_FF), f32) * (CONV_W ** -0.5)
    b_conv = 0.02 * jax.random.normal(ks[15], (2 * D_FF,), f32)
    w_down = jax.random.normal(ks[16], (D_FF, D_MODEL), f32) * (D_FF ** -0.5)
    norm_f_w = 1.0 + 0.02 * jax.random.normal(ks[17], (D_MODEL,), f32)
    return {"x": x, "c": c, "w_ada": w_ada, "b_ada": b_ada, "norm1_w": norm1_w,
            "w_in": w_in, "b_gates": b_gates, "mlstm_norm_w": mlstm_norm_w, "w_out": w_out,
            "norm2_w": norm2_w, "w_up": w_up, "w_conv": w_conv, "b_conv": b_conv,
            "w_down": w_down, "norm_f_w": norm_f_w}


def reference(x, c, w_ada, b_ada, norm1_w, w_in, b_gates, mlstm_norm_w, w_out,
              norm2_w, w_up, w_conv, b_conv, w_down, norm_f_w):
    mod = jax.nn.silu(c) @ w_ada + b_ada
    shift1, scale1, gate1, shift2, scale2, gate2 = jnp.split(mod, N_MOD, axis=-1)
    o1 = D_FOURIER
    o2 = o1 + D_QK
    o3 = o2 + D_QK
    o4 = o3 + D_MLSTM
    o5 = o4 + D_MLSTM
    for _ in range(DEPTH):
        h = modulate(rms_norm(x, norm1_w), shift1, scale1)
        p = h @ w_in
        u_f = p[..., :o1]
        q_in, k_in, v_in = p[..., o1:o2], p[..., o2:o3], p[..., o3:o4]
        o_gate = jax.nn.sigmoid(p[..., o4:o5])
        gates = p[..., o5:] + b_gates
        y_f = fourier_mix(u_f)
        y_m = mlstm_bidirectional(q_in, k_in, v_in, gates, mlstm_norm_w) * o_gate
        mix = jnp.concatenate([y_f, y_m], axis=-1)
        x = x + gate1[:, None, :] * (mix @ w_out)
        h2 = modulate(rms_norm(x, norm2_w), shift2, scale2)
        u = depthwise_conv_centred(h2 @ w_up, w_conv, b_conv)
        a, v = jnp.split(u, 2, axis=-1)
        x = x + gate2[:, None, :] * ((jax.nn.silu(a) * v) @ w_down)
    return rms_norm(x, norm_f_w)
```

```python
from contextlib import ExitStack

import numpy as np
import ml_dtypes
import concourse.bass as bass
import concourse.mybir as mybir
from concourse.bass_utils import run_bass_kernel_spmd

F32 = mybir.dt.float32
BF16 = mybir.dt.bfloat16
AF = mybir.ActivationFunctionType
ALU = mybir.AluOpType
AX = mybir.AxisListType

D = 2048
S = 4096
NB = 2
DFF = 5632
KC = 16
TOK = 1024
TOKH = TOK + 2
TT = 342
NCH = 44
ROUNDS = 4
CPR = NCH // ROUNDS
EPS = 1e-6


class Op:
    __slots__ = ("idx", "eng", "fn", "deps", "dma", "sig", "has_dep")

    def __init__(self, idx, eng, fn, deps, dma):
        self.idx, self.eng, self.fn, self.deps, self.dma = idx, eng, fn, deps, dma
        self.sig = None
        self.has_dep = False


class Prog:
    ENGS = ("pe", "act", "dve", "pool", "sp")

    def __init__(self, nc):
        self.nc = nc
        self.ops = []
        self.last_w = {}
        self.readers = {}

    def add(self, eng, fn, reads=(), writes=(), dma=None, preads=()):
        idx = len(self.ops)
        deps = {}
        for r in preads:
            for rd in self.readers.get(r, ()):
                if self.ops[rd].eng != eng:
                    deps[rd] = "raw"
        reads = list(reads) + list(preads)
        for r in reads:
            lw = self.last_w.get(r)
            if lw is not None:
                deps[lw] = "raw"
        for w in writes:
            lw = self.last_w.get(w)
            if lw is not None:
                deps.setdefault(lw, "waw")
            for rd in self.readers.get(w, ()):
                deps.setdefault(rd, "war")
        for r in reads:
            self.readers.setdefault(r, []).append(idx)
        for w in writes:
            self.last_w[w] = idx
            self.readers[w] = []
        keep = []
        for d, kind in deps.items():
            src = self.ops[d]
            if src.dma is None and src.eng == eng:
                if eng in ("pe", "sp"):
                    continue
                if kind == "war":
                    continue
            keep.append(d)
        op = Op(idx, eng, fn, keep, dma)
        self.ops.append(op)
        return idx

    def emit(self):
        nc = self.nc
        ops = self.ops
        for op in ops:
            for d in op.deps:
                ops[d].has_dep = True
        with ExitStack() as es:
            es.enter_context(nc.allow_low_precision(reason="bf16 matmul operands by design"))
            eng_sem = {e: es.enter_context(nc.semaphore("sem_" + e)) for e in ("pe", "act", "dve", "pool")}
            dma_sem = {}
            for op in ops:
                if op.dma is not None and op.dma not in dma_sem:
                    dma_sem[op.dma] = es.enter_context(nc.semaphore("dsem%d" % len(dma_sem)))
            eng_cnt = {e: 0 for e in eng_sem}
            dma_cnt = {k: 0 for k in dma_sem}
            for op in ops:
                if op.fn is None:
                    continue
                if op.dma is not None:
                    dma_cnt[op.dma] += 16
                    op.sig = (dma_sem[op.dma], dma_cnt[op.dma], 16)
                elif op.has_dep:
                    eng_cnt[op.eng] += 1
                    op.sig = (eng_sem[op.eng], eng_cnt[op.eng], 1)
            per_eng = {e: [op for op in ops if op.eng == e] for e in self.ENGS}
            block = es.enter_context(nc.Block())

            def run(e, lst):
                waited = {}
                for op in lst:
                    need = {}
                    for d in op.deps:
                        sem, val, _ = ops[d].sig
                        k = id(sem)
                        if k not in need or need[k][1] < val:
                            need[k] = (sem, val)
                    for k, (sem, val) in need.items():
                        if waited.get(k, 0) >= val:
                            continue
                        waited[k] = val
                        e.wait_ge(sem, val)
                    if op.fn is None:
                        continue
                    ins = op.fn(e)
                    if op.sig is not None:
                        ins.then_inc(op.sig[0], op.sig[2])

            @block.tensor
            def _(e):
                run(e, per_eng["pe"])

            @block.scalar
            def _(e):
                run(e, per_eng["act"])

            @block.vector
            def _(e):
                run(e, per_eng["dve"])

            @block.gpsimd
            def _(e):
                run(e, per_eng["pool"])

            @block.sync
            def _(e):
                run(e, per_eng["sp"])


class Arena:
    def __init__(self, t, nbytes):
        self.t = t
        self.nbytes = nbytes
        self.off = 0

    def at(self, off, nelem, dtype):
        sz = 4 if dtype == F32 else 2
        assert off % 4 == 0 and off + nelem * sz <= self.nbytes, (off, nelem, self.nbytes)
        ap = self.t[:, off // 2: off // 2 + nelem * sz // 2]
        if dtype == F32:
            ap = ap.bitcast(F32)
        return ap

    def alloc(self, nelem, dtype):
        sz = 4 if dtype == F32 else 2
        off = self.off
        nb = (nelem * sz + 31) // 32 * 32
        self.off += nb
        return self.at(off, nelem, dtype), off


class WStream:
    def __init__(self, P, name, slots, prefetch):
        self.P, self.name, self.slots, self.pf = P, name, slots, prefetch
        self.items = []
        self.issued = 0

    def push(self, src_ap, n):
        self.items.append((src_ap, n))
        return len(self.items) - 1

    def need(self, i):
        tgt = min(i + self.pf, len(self.items) - 1)
        while self.issued <= tgt:
            j = self.issued
            s = j % len(self.slots)
            src, n = self.items[j]
            dst = self.slots[s][:, 0:n]
            self.P.add("pool", lambda e, dst=dst, src=src: e.dma_start(out=dst, in_=src),
                       writes=[(self.name, s)], dma=(self.name, s))
            self.issued += 1
        s = i % len(self.slots)
        return self.slots[s], (self.name, s)


def emit_mod(P, ps, dr, mods, ws, base, sc_bf, m_list, bank):
    psb = ps[:, 512 * bank: 512 * bank + 96]
    for j, m in enumerate(m_list):
        slot, key = ws.need(base + j)
        for k in range(KC):
            P.add("pe", lambda e, m=m, k=k, slot=slot: e.matmul(
                psb[:, m:m + 1], lhsT=slot[:, 128 * k:128 * k + 128], rhs=sc_bf[:, k:k + 1],
                start=(k == 0), stop=(k == KC - 1)),
                reads=[key, "sc_bf"], writes=[("ps", bank)])
    m0, m1 = m_list[0], m_list[-1] + 1
    P.add("dve", lambda e: e.tensor_tensor(out=mods[:, m0:m1], in0=psb[:, m0:m1], in1=dr["bada_sb"][:, m0:m1],
                                           op=ALU.add),
          reads=[("ps", bank), "bada"], writes=["mods"])


def build_phase_b(nc, fused_inputs=None):
    P = Prog(nc)
    dr = {}

    def din(name, shape, dt=F32):
        dr[name] = nc.dram_tensor(name, shape, dt, kind="ExternalInput").ap()

    din("xTb", [128, KC * TOKH])
    din("mixT", [128, KC * TOKH], BF16)
    din("wout", [KC, 128, 2048])
    din("wup", [2 * NCH, 128, 2048])
    din("wdn", [ROUNDS * KC, 128, CPR * 128])
    din("wcv", [128, 2 * NCH * 4])
    din("cvec", [128, KC])
    din("wada", [96, 128, 2048])
    din("bada", [128, 96])
    din("nw", [128, 3 * KC])
    din("hmask", [128, 2])
    dr["outT"] = nc.dram_tensor("outT", [128, KC * TOK], F32, kind="ExternalOutput").ap()

    NBYTES = 206 * 1024
    es = ExitStack()
    arena_t = es.enter_context(nc.sbuf_tensor("arena", [128, NBYTES // 2], BF16))
    ps = es.enter_context(nc.psum_tensor("ps", [128, 4096], F32))
    A = Arena(arena_t, NBYTES)

    x1T, _ = A.alloc(KC * TOKH, F32)
    h2T, _ = A.alloc(KC * TOKH, BF16)
    NWS = 6
    wslots = [A.alloc(2048, BF16)[0] for _ in range(NWS)]
    NDS = 4
    dslots = [A.alloc(CPR * 128, BF16)[0] for _ in range(NDS)]
    small, _ = A.alloc(96 + 96 + 3 * KC + 2 * NCH * 4 + KC + KC + 2 + 8, F32)
    mods = small[:, 0:96]
    bada_sb = small[:, 96:192]
    nw_sb = small[:, 192:240]
    wcv_sb = small[:, 240:240 + 352]
    cvec_sb = small[:, 592:608]
    a2_sb = small[:, 608:624]
    hm_sb = small[:, 624:626]
    dr["bada_sb"] = bada_sb
    misc, _ = A.alloc(KC + 128, BF16)
    sc_bf = misc[:, 0:KC]
    ones_bf = misc[:, KC:KC + 128]
    rstd, _ = A.alloc(TOKH, F32)
    sqb = [A.alloc(TOKH, BF16)[0] for _ in range(2)]
    shared_off = A.off
    mixT = A.at(shared_off, KC * TOKH, BF16)
    gT = A.at(shared_off, CPR * TOK, BF16)
    yoff = shared_off + CPR * TOK * 2
    ybuf = [[A.at(yoff + (2 * i + j) * 4128, TOKH, F32) for j in range(2)] for i in range(2)]
    toff = yoff + 4 * 4128
    tbuf = [[A.at(toff + (2 * i + j) * 4096, TOK, F32) for j in range(2)] for i in range(2)]
    tmpb = [A.at(toff + 4 * 4096 + i * 4128, TOKH, F32) for i in range(2)]
    end = max(shared_off + KC * TOKH * 2, toff + 4 * 4096 + 2 * 4128)
    assert end <= NBYTES, end

    x1v = x1T.rearrange("p (k t) -> p k t", k=KC)
    h2v = h2T.rearrange("p (k t) -> p k t", k=KC)
    mixv = mixT.rearrange("p (k t) -> p k t", k=KC)
    gv = gT.rearrange("p (c t) -> p c t", c=CPR)

    def ld(dst, src, key):
        P.add("sp", lambda e: e.dma_start(out=dst, in_=src), writes=[key], dma=key)

    ld(cvec_sb, dr["cvec"], "cvec")
    ld(bada_sb, dr["bada"], "bada")
    ld(nw_sb, dr["nw"], "nw")
    ld(wcv_sb, dr["wcv"], "wcv")
    ld(hm_sb, dr["hmask"], "hmask")
    for k in range(KC):
        P.add("sp", lambda e, k=k: e.dma_start(out=mixv[:, k, :], in_=dr["mixT"][:, k * TOKH:(k + 1) * TOKH]),
              writes=[("mixT", k)], dma=("mixT", k))
    for k in range(KC):
        P.add("sp", lambda e, k=k: e.dma_start(out=x1v[:, k, :], in_=dr["xTb"][:, k * TOKH:(k + 1) * TOKH]),
              writes=[("x1T", k)], dma=("x1T", k))
    P.add("dve", lambda e: e.memset(ones_bf, 1.0), writes=["ones"])
    P.add("act", lambda e: e.activation(out=sc_bf, in_=cvec_sb, func=AF.Silu), reads=["cvec"], writes=["sc_bf"])

    ws = WStream(P, "w", wslots, NWS - 2)
    mlist = list(range(32, 96))
    for m in mlist:
        ws.push(dr["wada"][m], 2048)
    wo_base = len(ws.items)
    for m in range(KC):
        ws.push(dr["wout"][m], 2048)
    wu_base = len(ws.items)
    for g in range(2 * NCH):
        ws.push(dr["wup"][g], 2048)
    emit_mod(P, ps, dr, mods, ws, 0, sc_bf, mlist, 7)
    G1, SH2, SC2, G2 = 32, 48, 64, 80
    P.add("dve", lambda e: e.scalar_tensor_tensor(out=a2_sb, in0=mods[:, SC2:SC2 + KC], scalar=1.0,
                                                  in1=nw_sb[:, KC:2 * KC], op0=ALU.add, op1=ALU.mult),
          reads=["mods", "nw"], writes=["a2"])

    tiles3 = [(i * TT, TT) for i in range(3)]
    for m in range(KC):
        slot, key = ws.need(wo_base + m)
        bset = (m % 2) * 3
        for ti, (t0, tn) in enumerate(tiles3):
            bank = bset + ti
            for k in range(KC):
                P.add("pe", lambda e, bank=bank, k=k, slot=slot, t0=t0, tn=tn: e.matmul(
                    ps[:, 512 * bank:512 * bank + tn], lhsT=slot[:, 128 * k:128 * k + 128],
                    rhs=mixv[:, k, t0:t0 + tn], start=(k == 0), stop=(k == KC - 1)),
                    reads=[key, ("mixT", k)], writes=[("ps", bank)])
            P.add("dve", lambda e, bank=bank, m=m, t0=t0, tn=tn: e.scalar_tensor_tensor(
                out=x1v[:, m, t0:t0 + tn], in0=ps[:, 512 * bank:512 * bank + tn], scalar=mods[:, G1 + m:G1 + m + 1],
                in1=x1v[:, m, t0:t0 + tn], op0=ALU.mult, op1=ALU.add),
                reads=[("ps", bank), "mods", ("x1T", m)], writes=[("x1T", m)])

    def emit_rstd(lo, tiles, banks):
        for k in range(KC):
            sq = sqb[k % 2]
            n = tiles[-1][0] + tiles[-1][1]
            P.add("act", lambda e, k=k, sq=sq, n=n: e.activation(out=sq[:, 0:n], in_=x1v[:, k, lo:lo + n], func=AF.Square),
                  reads=[("x1T", k)], writes=[("sq", k % 2)])
            for (t0, tn), bank in zip(tiles, banks):
                P.add("pe", lambda e, k=k, sq=sq, t0=t0, tn=tn, bank=bank: e.matmul(
                    ps[:, 512 * bank:512 * bank + tn], lhsT=ones_bf, rhs=sq[:, t0:t0 + tn],
                    start=(k == 0), stop=(k == KC - 1)),
                    reads=[("sq", k % 2), "ones"], writes=[("ps", bank)])
        for (t0, tn), bank in zip(tiles, banks):
            P.add("act", lambda e, t0=t0, tn=tn, bank=bank: e.activation(
                out=rstd[:, t0:t0 + tn], in_=ps[:, 512 * bank:512 * bank + tn], func=AF.Sqrt,
                bias=eps_sb, scale=1.0 / D),
                reads=[("ps", bank), "eps"], writes=[("rstd", t0)])
            P.add("dve", lambda e, t0=t0, tn=tn: e.reciprocal(out=rstd[:, t0:t0 + tn], in_=rstd[:, t0:t0 + tn]),
                  reads=[("rstd", t0)], writes=[("rstd", t0)])

    eps_sb = small[:, 626:627]
    P.add("dve", lambda e: e.memset(eps_sb, EPS), writes=["eps"])

    emit_rstd(0, tiles3, [6, 7, 0])
    rkeys3 = [("rstd", t0) for t0, _ in tiles3]
    for k in range(KC):
        tmp = tmpb[k % 2]
        P.add("dve", lambda e, k=k, tmp=tmp: e.scalar_tensor_tensor(
            out=tmp, in0=x1v[:, k, :], scalar=a2_sb[:, k:k + 1], in1=rstd, op0=ALU.mult, op1=ALU.mult),
            reads=[("x1T", k), "a2"] + rkeys3, writes=[("tmp", k % 2)])
        P.add("act", lambda e, k=k, tmp=tmp: e.activation(
            out=h2v[:, k, :], in_=tmp, func=AF.Identity, bias=mods[:, SH2 + k:SH2 + k + 1], scale=1.0),
            reads=[("tmp", k % 2), "mods"], writes=[("h2T", k)])
    for side, col in ((0, 0), (1, TOKH - 1)):
        P.add("dve", lambda e, side=side, col=col: e.tensor_scalar(
            out=h2v[:, :, col:col + 1], in0=h2v[:, :, col:col + 1], scalar1=hm_sb[:, side:side + 1], scalar2=None,
            op0=ALU.mult),
            reads=[("h2T", k) for k in range(KC)] + ["hmask"], writes=[("h2T", k) for k in range(KC)])

    wd = WStream(P, "wd", dslots, NDS - 1)
    for r in range(ROUNDS):
        for o in range(KC):
            wd.push(dr["wdn"][r * KC + o], CPR * 128)

    wcv = wcv_sb.rearrange("p (g f) -> p g f", f=4)
    grp = 0
    for r in range(ROUNDS):
        for cc in range(CPR):
            c = r * CPR + cc
            par = c % 2
            for av in range(2):
                g = 2 * c + av
                slot, key = ws.need(wu_base + g)
                bset = (grp % 2) * 3
                grp += 1
                y = ybuf[par][av]
                for ti, (t0, tn) in enumerate(tiles3):
                    bank = bset + ti
                    for k in range(KC):
                        P.add("pe", lambda e, bank=bank, k=k, slot=slot, t0=t0, tn=tn: e.matmul(
                            ps[:, 512 * bank:512 * bank + tn], lhsT=slot[:, 128 * k:128 * k + 128],
                            rhs=h2v[:, k, t0:t0 + tn], start=(k == 0), stop=(k == KC - 1)),
                            reads=[key, ("h2T", k)], writes=[("ps", bank)])
                    P.add("act", lambda e, bank=bank, y=y, t0=t0, tn=tn: e.activation(
                        out=y[:, t0:t0 + tn], in_=ps[:, 512 * bank:512 * bank + tn], func=AF.Copy),
                        reads=[("ps", bank)], writes=[("y", par, av, ti)])
                t = tbuf[par][av]
                P.add("dve", lambda e, y=y, t=t, g=g: e.tensor_scalar(
                    out=t, in0=y[:, 0:TOK], scalar1=wcv[:, g, 0:1], scalar2=wcv[:, g, 3:4], op0=ALU.mult, op1=ALU.add),
                    reads=[("y", par, av, 0), ("y", par, av, 1), ("y", par, av, 2), "wcv"], writes=[("t", par, av)])
                for j in (1, 2):
                    P.add("dve", lambda e, y=y, t=t, g=g, j=j: e.scalar_tensor_tensor(
                        out=t, in0=y[:, j:j + TOK], scalar=wcv[:, g, j:j + 1], in1=t, op0=ALU.mult, op1=ALU.add),
                        reads=[("y", par, av, 0), ("y", par, av, 1), ("y", par, av, 2), ("t", par, av), "wcv"],
                        writes=[("t", par, av)])
            ta, tv = tbuf[par]
            P.add("act", lambda e, ta=ta: e.activation(out=ta, in_=ta, func=AF.Silu),
                  reads=[("t", par, 0)], writes=[("t", par, 0)])
            P.add("dve", lambda e, ta=ta, tv=tv, cc=cc: e.tensor_tensor(out=gv[:, cc, :], in0=ta, in1=tv, op=ALU.mult),
                  reads=[("t", par, 0), ("t", par, 1)], writes=[("gT", cc)])
        for o in range(KC):
            slot, key = wd.need(r * KC + o)
            for tt in range(2):
                bank = (2 * o + tt) % 8
                for cc in range(CPR):
                    P.add("pe", lambda e, bank=bank, cc=cc, slot=slot, tt=tt: e.matmul(
                        ps[:, 512 * bank:512 * bank + 512], lhsT=slot[:, 128 * cc:128 * cc + 128],
                        rhs=gv[:, cc, 512 * tt:512 * tt + 512], start=(cc == 0), stop=(cc == CPR - 1)),
                        reads=[key, ("gT", cc)], writes=[("ps", bank)])
                P.add("dve", lambda e, bank=bank, o=o, tt=tt: e.scalar_tensor_tensor(
                    out=x1v[:, o, 1 + 512 * tt:1 + 512 * tt + 512], in0=ps[:, 512 * bank:512 * bank + 512],
                    scalar=mods[:, G2 + o:G2 + o + 1], in1=x1v[:, o, 1 + 512 * tt:1 + 512 * tt + 512],
                    op0=ALU.mult, op1=ALU.add),
                    reads=[("ps", bank), "mods", ("x1T", o)], writes=[("x1T", o)])

    tiles2 = [(0, 512), (512, 512)]
    emit_rstd(1, tiles2, [0, 1])
    P.add("dve", lambda e: e.tensor_copy(out=a2_sb, in_=nw_sb[:, 2 * KC:3 * KC]), reads=["nw", "a2"], writes=["a2"])
    obufs = [tbuf[0][0], tbuf[0][1], tbuf[1][0], tbuf[1][1]]
    for k in range(KC):
        ob = obufs[k % 4]
        okey = ("t", (k % 4) // 2, (k % 4) % 2)
        P.add("dve", lambda e, k=k, ob=ob: e.scalar_tensor_tensor(
            out=ob, in0=x1v[:, k, 1:1 + TOK], scalar=a2_sb[:, k:k + 1], in1=rstd[:, 0:TOK], op0=ALU.mult, op1=ALU.mult),
            reads=[("x1T", k), "a2", ("rstd", 0), ("rstd", 512)], writes=[okey])
        P.add("sp", lambda e, k=k, ob=ob: e.dma_start(out=dr["outT"][:, k * TOK:(k + 1) * TOK], in_=ob),
              reads=[okey], writes=[("out", k)], dma=("out", k % 4))
    P.add("sp", None, reads=[("out", k) for k in range(KC)])
    P.emit()
    es.close()
    return nc


def prep_phase_b_weights(inp):
    f = np.float32
    w_out = np.asarray(inp["w_out"], f)
    wout = np.ascontiguousarray(w_out.reshape(KC, 128, KC, 128).transpose(2, 1, 0, 3)).reshape(KC, 128, 2048)
    w_up = np.asarray(inp["w_up"], f)
    wu = w_up.reshape(KC, 128, 2, NCH, 128)
    wup = np.ascontiguousarray(wu.transpose(3, 2, 1, 0, 4)).reshape(2 * NCH, 128, 2048)
    w_dn = np.asarray(inp["w_down"], f)
    wd = w_dn.reshape(ROUNDS, CPR, 128, KC, 128)
    wdn = np.ascontiguousarray(wd.transpose(0, 3, 2, 1, 4)).reshape(ROUNDS * KC, 128, CPR * 128)
    w_conv = np.asarray(inp["w_conv"], f)
    b_conv = np.asarray(inp["b_conv"], f)
    cw = np.concatenate([w_conv, b_conv[None]], 0)
    cw = cw.reshape(4, 2, NCH, 128)
    wcv = np.ascontiguousarray(cw.transpose(3, 2, 1, 0)).reshape(128, 2 * NCH * 4)
    w_ada = np.asarray(inp["w_ada"], f)
    wada = np.ascontiguousarray(w_ada.reshape(KC, 128, 96, 128).transpose(2, 1, 0, 3)).reshape(96, 128, 2048)
    bada = np.ascontiguousarray(np.asarray(inp["b_ada"], f).reshape(96, 128).T)
    nw = np.concatenate([np.asarray(inp[n], f).reshape(KC, 128).T for n in ("norm1_w", "norm2_w", "norm_f_w")], 1)
    return dict(wout=wout, wup=wup, wdn=wdn, wcv=wcv, wada=wada, bada=bada, nw=np.ascontiguousarray(nw))


def phase_b_core_inputs(inp, mix, shared, b, j):
    f = np.float32
    x = np.asarray(inp["x"], f)
    lo, hi = TOK * j - 1, TOK * j + TOK + 1
    xs = np.zeros((TOKH, D), f)
    ms = np.zeros((TOKH, D), ml_dtypes.bfloat16)
    a, bb = max(lo, 0), min(hi, S)
    xs[a - lo:bb - lo] = x[b, a:bb]
    ms[a - lo:bb - lo] = mix[b, a:bb]
    xT = np.ascontiguousarray(xs.reshape(TOKH, KC, 128).transpose(2, 1, 0)).reshape(128, KC * TOKH)
    mT = np.ascontiguousarray(ms.reshape(TOKH, KC, 128).transpose(2, 1, 0)).reshape(128, KC * TOKH)
    cvec = np.ascontiguousarray(np.asarray(inp["c"], f)[b].reshape(KC, 128).T)
    hm = np.ones((128, 2), f)
    if j == 0:
        hm[:, 0] = 0
    if j == S // TOK - 1:
        hm[:, 1] = 0
    d = dict(xTb=xT, mixT=mT, cvec=cvec, hmask=hm)
    d.update(shared)
    return d


def run_phase_b(inp, mix):
    nc = build_phase_b(bass.Bass("TRN2", target_bir_lowering=False))
    shared = prep_phase_b_weights(inp)
    in_maps = []
    for core in range(8):
        b, j = divmod(core, 4)
        in_maps.append(phase_b_core_inputs(inp, mix, shared, b, j))
    res = run_bass_kernel_spmd(nc, in_maps, core_ids=list(range(8)))
    out = np.zeros((NB, S, D), np.float32)
    for core in range(8):
        b, j = divmod(core, 4)
        oT = res.results[core]["outT"].reshape(128, KC, TOK)
        out[b, TOK * j:TOK * (j + 1)] = oT.transpose(2, 1, 0).reshape(TOK, D)
    return out


NSUB = S // 128
NTILE = S // 256
NCOL = 1028
VW = 258
DFG = 4


def build_phase_a(nc):
    P = Prog(nc)
    dr = {}

    def din(name, shape, dt=F32):
        dr[name] = nc.dram_tensor(name, shape, dt, kind="ExternalInput").ap()

    din("xTa", [NTILE, 128, KC * 256])
    din("winA", [KC, 128, NCOL])
    din("bg", [128, 4])
    din("mnw", [128, 2])
    din("cvec", [128, KC])
    din("wada", [96, 128, 2048])
    din("bada", [128, 96])
    din("nw", [128, 3 * KC])
    din("cf32", [128, 3 * 128])
    din("identb", [128, 128], BF16)
    din("cdft", [128, 2 * 2 * 256], BF16)
    din("dft", [8 * (32 // DFG), 128, 2 * DFG * 512], BF16)
    dr["mixA"] = nc.dram_tensor("mixA", [128, 4 * S], BF16, kind="ExternalOutput").ap()

    NBYTES = 206 * 1024
    es = ExitStack()
    arena_t = es.enter_context(nc.sbuf_tensor("arena", [128, NBYTES // 2], BF16))
    ps = es.enter_context(nc.psum_tensor("ps", [128, 4096], F32))
    A = Arena(arena_t, NBYTES)

    Usb, _ = A.alloc(NSUB * 256, BF16)
    qkT, _ = A.alloc(4 * S, BF16)
    ktm, _ = A.alloc(NSUB * 256, BF16)
    vext, _ = A.alloc(NSUB * VW, BF16)
    sigo, _ = A.alloc(NSUB * 256, BF16)
    SC, _ = A.alloc(NSUB * 8, F32)
    cf32, _ = A.alloc(3 * 128, F32)
    identb, _ = A.alloc(128, BF16)
    cdft, _ = A.alloc(1024, BF16)
    small, _ = A.alloc(256, F32)
    mods = small[:, 0:96]
    bada_sb = small[:, 96:192]
    dr["bada_sb"] = bada_sb
    cvec_sb = small[:, 192:208]
    a1_sb = small[:, 208:224]
    bg_sb = small[:, 224:228]
    mnw_sb = small[:, 228:230]
    cst = small[:, 230:234]
    nw_sb, _ = A.alloc(3 * KC, F32)
    misc, _ = A.alloc(KC + 128, BF16)
    sc_bf = misc[:, 0:KC]
    ones_bf = misc[:, KC:KC + 128]
    Uv = Usb.rearrange("p (s c) -> p s c", s=NSUB)
    qkv = qkT.rearrange("p (k t) -> p k t", k=4)
    ktv = ktm.rearrange("p (s d c) -> p s d c", s=NSUB, d=2)
    vv = vext.rearrange("p (s c) -> p s c", s=NSUB)
    sov = sigo.rearrange("p (s c) -> p s c", s=NSUB)
    SCv = SC.rearrange("p (s c) -> p s c", s=NSUB)
    triF, triB, ones_f = cf32[:, 0:128], cf32[:, 128:256], cf32[:, 256:384]
    cdv = cdft.rearrange("p (a b c) -> p a b c", a=2, b=2)

    roff = A.off
    RSIZE = NBYTES - roff
    o = roff
    Wp = A.at(o, KC * NCOL, BF16); o += KC * NCOL * 2
    r0 = A.at(o, NCOL, F32); o += (NCOL * 4 + 31) // 32 * 32
    tq = [A.at(o + i * 512, 128, F32) for i in range(2)]; o += 1024
    tk = [A.at(o + i * 512, 128, F32) for i in range(2)]; o += 1024
    to = [A.at(o + i * 1024, 256, F32) for i in range(2)]; o += 2048
    eo = [A.at(o + i * 1024, 256, F32) for i in range(2)]; o += 2048
    qtm = [A.at(o + i * 512, 256, BF16) for i in range(2)]; o += 1024
    g4 = [A.at(o + i * 32, 4, F32) for i in range(2)]; o += 64
    nlf = [A.at(o + i * 32, 2, F32) for i in range(2)]; o += 64
    e1 = [A.at(o + i * 32, 2, F32) for i in range(2)]; o += 64
    tkk = [A.at(o + i * 32, 2, F32) for i in range(2)]; o += 64
    lnv = [A.at(o + i * 32, 1, F32) for i in range(2)]; o += 64
    o2 = o
    wslots = [A.at(o2 + i * 4096, 2048, BF16) for i in range(3)]; o2 += 3 * 4096
    stage = [A.at(o2 + i * 4128, NCOL, F32) for i in range(2)]; o2 += 2 * 4128
    wtmp = [A.at(o2 + i * 2080, NCOL, BF16) for i in range(2)]; o2 += 2 * 2080
    shbc = A.at(o2, KC * 128, BF16); o2 += KC * 128 * 2
    o3 = o
    xt = [A.at(o3 + i * 8192, KC * 256, BF16) for i in range(3)]; o3 += 3 * 8192
    sq = [A.at(o3 + i * 8192, KC * 256, BF16) for i in range(2)]; o3 += 2 * 8192
    assert max(o2, o3) <= NBYTES, (o2, o3)
    o = roff
    NDB = 4
    dbuf = [A.at(o + i * 8192, 2 * DFG * 512, BF16) for i in range(NDB)]; o += NDB * 8192
    hsum = A.at(o, NSUB * 256, F32); o += NSUB * 256 * 4
    GT = [A.at(o + i * 4096, 4 * 512, BF16) for i in range(2)]; o += 2 * 4096
    yfs = [A.at(o + i * 2048, 2 * 512, BF16) for i in range(2)]; o += 2 * 2048
    yms = [A.at(o + i * 512, 2 * 128, BF16) for i in range(2)]; o += 2 * 512
    yb = [A.at(o + i * 512, 256, BF16) for i in range(2)]; o += 2 * 512
    Dpre = [A.at(o + i * 1056, VW, F32) for i in range(2)]; o += 2 * 1056
    Cbf = [A.at(o + i * 544, VW, BF16) for i in range(2)]; o += 2 * 544
    ptb = [A.at(o + i * 256, 128, BF16) for i in range(2)]; o += 2 * 256
    rr = [A.at(o + i * 32, 2, F32) for i in range(4)]; o += 4 * 32
    msn = [A.at(o + i * 32, 2, F32) for i in range(2)]; o += 2 * 32
    assert o <= NBYTES, o
    hv = hsum.rearrange("p (s c) -> p s c", s=NSUB)
    Wpv = Wp.rearrange("p (k c) -> p k c", k=KC)
    shv = shbc.rearrange("p (k c) -> p k c", k=KC)

    def ld(dst, src, key):
        P.add("sp", lambda e: e.dma_start(out=dst, in_=src), writes=[key], dma=key)

    ld(cvec_sb, dr["cvec"], "cvec")
    ld(bada_sb, dr["bada"], "bada")
    ld(nw_sb, dr["nw"], "nw")
    ld(bg_sb, dr["bg"], "bg")
    ld(mnw_sb, dr["mnw"], "mnw")
    ld(cf32, dr["cf32"], "cf32")
    ld(identb, dr["identb"], "identb")
    ld(cdft, dr["cdft"], "cdft")
    P.add("dve", lambda e: e.memset(ones_bf, 1.0), writes=["ones"])
    P.add("dve", lambda e: e.memset(cst[:, 0:1], EPS), writes=["cst"])
    P.add("dve", lambda e: e.memset(cst[:, 1:2], 1.0), writes=["cst"])
    P.add("dve", lambda e: e.memset(vv[:, :, 256:258], 1.0), writes=["vones"])
    P.add("act", lambda e: e.activation(out=small[:, 240:256], in_=cvec_sb, func=AF.Exp, scale=-1.0),
          reads=["cvec"], writes=["sc_tmp"])
    P.add("dve", lambda e: e.tensor_scalar(out=small[:, 240:256], in0=small[:, 240:256], scalar1=1.0, scalar2=None,
                                           op0=ALU.add), reads=["sc_tmp"], writes=["sc_tmp"])
    P.add("dve", lambda e: e.reciprocal(out=small[:, 240:256], in_=small[:, 240:256]), reads=["sc_tmp"], writes=["sc_tmp"])
    P.add("dve", lambda e: e.tensor_tensor(out=sc_bf, in0=small[:, 240:256], in1=cvec_sb, op=ALU.mult),
          reads=["sc_tmp", "cvec"], writes=["sc_bf"])

    ws = WStream(P, "w", wslots, 2)
    mlist = list(range(0, 32))
    for m in mlist:
        ws.push(dr["wada"][m], 2048)
    emit_mod(P, ps, dr, mods, ws, 0, sc_bf, mlist, 6)
    SH1, SC1 = 0, 16
    P.add("dve", lambda e: e.scalar_tensor_tensor(out=a1_sb, in0=mods[:, SC1:SC1 + KC], scalar=1.0,
                                                  in1=nw_sb[:, 0:KC], op0=ALU.add, op1=ALU.mult),
          reads=["mods", "nw"], writes=["a1"])
    for k in range(KC):
        P.add("dve", lambda e, k=k: e.tensor_scalar(out=shv[:, k, :], in0=ones_f, scalar1=mods[:, SH1 + k:SH1 + k + 1],
                                                    scalar2=None, op0=ALU.mult),
              reads=["mods", "cf32"], writes=[("shbc", k)])

    pieces = [(0, 512, 0), (512, 512, 1), (1024, 4, 2)]
    for k in range(KC):
        st = stage[k % 2]
        P.add("sp", lambda e, k=k, st=st: e.dma_start(out=st, in_=dr["winA"][k]), writes=[("stage", k % 2)],
              dma=("stage", k % 2))
        P.add("dve", lambda e, k=k, st=st: e.tensor_scalar(out=Wpv[:, k, :], in0=st, scalar1=a1_sb[:, k:k + 1],
                                                           scalar2=None, op0=ALU.mult),
              reads=[("stage", k % 2), "a1"], writes=[("Wp", k)])
        wt = wtmp[k % 2]
        P.add("act", lambda e, st=st, wt=wt: e.activation(out=wt, in_=st, func=AF.Copy),
              reads=[("stage", k % 2)], writes=[("wtmp", k % 2)])
        for c0, n, bank in pieces:
            P.add("pe", lambda e, k=k, wt=wt, c0=c0, n=n, bank=bank: e.matmul(
                ps[:, 512 * bank:512 * bank + n], lhsT=shv[:, k, :], rhs=wt[:, c0:c0 + n],
                start=(k == 0), stop=(k == KC - 1)),
                reads=[("wtmp", k % 2), ("shbc", k)], writes=[("ps", bank)])
    for c0, n, bank in pieces[:2]:
        P.add("act", lambda e, c0=c0, n=n, bank=bank: e.activation(out=r0[:, c0:c0 + n], in_=ps[:, 512 * bank:512 * bank + n],
                                                                   func=AF.Copy),
              reads=[("ps", bank)], writes=[("r0", bank)])
    P.add("dve", lambda e: e.tensor_tensor(out=r0[:, 1024:1028], in0=ps[:, 1024:1028], in1=bg_sb, op=ALU.add),
          reads=[("ps", 2), "bg"], writes=[("r0", 2)])

    QSC = float(128 ** -0.5)
    R0K = [("r0", 0), ("r0", 1), ("r0", 2)]
    tiles = {}

    def stage1(s):
        T, sub = divmod(s, 2)
        par = s % 2
        if sub == 0:
            xb = xt[T % 3]
            xkey = ("xt", T % 3)
            P.add("pool", lambda e: e.dma_start(out=xb, in_=dr["xTa"][T]), reads=R0K if T == 0 else [],
                  writes=[xkey], dma=xkey)
            sqb_ = sq[T % 2]
            skey = ("sq", T % 2)
            P.add("act", lambda e: e.activation(out=sqb_, in_=xb, func=AF.Square), reads=[xkey], writes=[skey])
            tiles[T] = (xb.rearrange("p (k t) -> p k t", k=KC), sqb_.rearrange("p (k t) -> p k t", k=KC), xkey, skey)
        xv, sv, xkey, skey = tiles[T]
        tsl = slice(sub * 128, sub * 128 + 128)
        bset = par * 3
        ssc = 512 * (bset + 2) + 8
        for k in range(KC):
            P.add("pe", lambda e, k=k: e.matmul(
                ps[:, ssc:ssc + 1], lhsT=sv[:, k, tsl], rhs=ones_bf[:, 0:1],
                start=(k == 0), stop=(k == KC - 1)),
                reads=[skey, "ones"], writes=[("ps", bset + 2)])
        for k in range(KC):
            for c0, n, pb in pieces:
                bank = bset + pb
                P.add("pe", lambda e, k=k, c0=c0, n=n, bank=bank: e.matmul(
                    ps[:, 512 * bank:512 * bank + n], lhsT=xv[:, k, tsl], rhs=Wpv[:, k, c0:c0 + n],
                    start=(k == 0), stop=(k == KC - 1)),
                    reads=[xkey, ("Wp", k)], writes=[("ps", bank)])
        b2 = 512 * (bset + 2)
        rstd = SCv[:, s, 6:7]
        P.add("act", lambda e: e.activation(out=lnv[par], in_=ps[:, ssc:ssc + 1], func=AF.Ln,
                                            bias=cst[:, 0:1], scale=1.0 / D),
              reads=["cst"], preads=[("ps", bset + 2)], writes=[("lnv", par)])
        P.add("act", lambda e: e.activation(out=rstd, in_=lnv[par], func=AF.Exp, scale=-0.5),
              reads=[("lnv", par)], writes=[("rstd", s)])
        P.add("dve", lambda e: e.scalar_tensor_tensor(
            out=g4[par], in0=ps[:, b2:b2 + 4], scalar=rstd, in1=r0[:, 1024:1028], op0=ALU.mult, op1=ALU.add),
            reads=[("rstd", s), ("r0", 2)], preads=[("ps", bset + 2)], writes=[("g4", par)])
        g4v = g4[par].rearrange("p (a b) -> p a b", b=2)
        P.add("act", lambda e: e.activation(out=e1[par], in_=g4v[:, :, 1], func=AF.Exp, scale=-1.0),
              reads=[("g4", par)], writes=[("e1", par)])
        P.add("act", lambda e: e.activation(out=nlf[par], in_=e1[par], func=AF.Ln, bias=cst[:, 1:2], scale=1.0),
              reads=[("e1", par), "cst"], writes=[("nlf", par)])

    def stage2(s):
        par = s % 2
        bset = par * 3
        b0, b1 = 512 * bset, 512 * (bset + 1)
        rstd = SCv[:, s, 6:7]
        g4v = g4[par].rearrange("p (a b) -> p a b", b=2)
        gc = 3072 + 8 + 4 * par
        gkey = ("ps", 6)
        P.add("pe", lambda e: e.matmul(ps[:, gc:gc + 1], lhsT=triF, rhs=nlf[par][:, 0:1], start=True, stop=True),
              reads=[("nlf", par), "cf32"], writes=[gkey])
        P.add("pe", lambda e: e.matmul(ps[:, gc + 1:gc + 2], lhsT=triB, rhs=nlf[par][:, 1:2], start=True, stop=True),
              reads=[("nlf", par), "cf32"], writes=[gkey])
        P.add("pe", lambda e: e.matmul(ps[:, gc + 2:gc + 4], lhsT=ones_f, rhs=nlf[par][:, 0:2], start=True, stop=True),
              reads=[("nlf", par), "cf32"], writes=[gkey])
        P.add("act", lambda e: e.activation(out=SCv[:, s, 0:4], in_=ps[:, gc:gc + 4], func=AF.Exp, scale=-1.0),
              preads=[gkey], writes=[("SC", s)])
        P.add("dve", lambda e: e.tensor_tensor(out=tkk[par], in0=ps[:, gc:gc + 2], in1=g4v[:, :, 0], op=ALU.add),
              reads=[("g4", par)], preads=[gkey], writes=[("tkk", par)])
        P.add("act", lambda e: e.activation(out=SCv[:, s, 4:6], in_=tkk[par], func=AF.Exp),
              reads=[("tkk", par)], writes=[("SCk", s)])
        P.add("dve", lambda e: e.scalar_tensor_tensor(
            out=Uv[:, s, :], in0=ps[:, b0:b0 + 256], scalar=rstd, in1=r0[:, 0:256], op0=ALU.mult, op1=ALU.add),
            reads=[("rstd", s), ("r0", 0)], preads=[("ps", bset)], writes=[("U", s)])
        P.add("dve", lambda e: e.scalar_tensor_tensor(
            out=to[par], in0=ps[:, b1 + 256:b1 + 512], scalar=rstd, in1=r0[:, 768:1024], op0=ALU.mult, op1=ALU.add),
            reads=[("rstd", s), ("r0", 1)], preads=[("ps", bset + 1)], writes=[("to", par)])
        P.add("act", lambda e: e.activation(out=eo[par], in_=to[par], func=AF.Exp, scale=-1.0),
              reads=[("to", par)], writes=[("eo", par)])
        P.add("dve", lambda e: e.scalar_tensor_tensor(
            out=tq[par], in0=ps[:, b0 + 256:b0 + 384], scalar=rstd, in1=r0[:, 256:384], op0=ALU.mult, op1=ALU.add),
            reads=[("rstd", s), ("r0", 0)], preads=[("ps", bset)], writes=[("tq", par)])
        P.add("dve", lambda e: e.scalar_tensor_tensor(
            out=tk[par], in0=ps[:, b0 + 384:b0 + 512], scalar=rstd, in1=r0[:, 384:512], op0=ALU.mult, op1=ALU.add),
            reads=[("rstd", s), ("r0", 0)], preads=[("ps", bset)], writes=[("tk", par)])
        P.add("dve", lambda e: e.scalar_tensor_tensor(
            out=vv[:, s, 0:256], in0=ps[:, b1:b1 + 256], scalar=rstd, in1=r0[:, 512:768], op0=ALU.mult, op1=ALU.add),
            reads=[("rstd", s), ("r0", 1)], preads=[("ps", bset + 1)], writes=[("v", s)])
        qv = qtm[par].rearrange("p (d c) -> p d c", d=2)
        for d_ in range(2):
            P.add("dve", lambda e, d_=d_: e.tensor_scalar(
                out=qv[:, d_, :], in0=tq[par], scalar1=SCv[:, s, d_:d_ + 1], scalar2=QSC, op0=ALU.mult, op1=ALU.mult),
                reads=[("tq", par), ("SC", s)], writes=[("qtm", par, d_)])
            P.add("dve", lambda e, d_=d_: e.tensor_scalar(
                out=ktv[:, s, d_, :], in0=tk[par], scalar1=SCv[:, s, 4 + d_:5 + d_], scalar2=None, op0=ALU.mult),
                reads=[("tk", par), ("SCk", s)], writes=[("ktm", s, d_)])
        P.add("dve", lambda e: e.tensor_scalar(out=eo[par], in0=eo[par], scalar1=1.0, scalar2=None, op0=ALU.add),
              reads=[("eo", par)], writes=[("eo", par)])
        P.add("dve", lambda e: e.reciprocal(out=sov[:, s, :], in_=eo[par]),
              reads=[("eo", par)], writes=[("sigo", s)])

    def stage3(s):
        par = s % 2
        qv = qtm[par].rearrange("p (d c) -> p d c", d=2)
        ptv = ps[:, 3584:4096].bitcast(BF16)[:, 512 * par:512 * par + 512]
        tkey = ("ps", 7)
        srcs = [(qv[:, 0, :], ("qtm", par, 0)), (qv[:, 1, :], ("qtm", par, 1)),
                (ktv[:, s, 0, :], ("ktm", s, 0)), (ktv[:, s, 1, :], ("ktm", s, 1))]
        for i, (src, key) in enumerate(srcs):
            P.add("pe", lambda e, i=i, src=src: e.transpose(ptv[:, 128 * i:128 * i + 128], src, identb),
                  reads=[key, "identb"], writes=[tkey])
        P.add("act", lambda e: e.activation(
            out=qkv[:, :, 128 * s:128 * s + 128], in_=ptv.rearrange("p (k t) -> p k t", k=4), func=AF.Copy),
            preads=[tkey], writes=[("qkT", s)])

    import os
    STOP = int(os.environ.get("PHASEA_STOP", "9"))
    NIT = NSUB if STOP >= 2 else (0 if STOP == 0 else int(os.environ.get("PHASEA_NIT", "4")))
    for it in range(NIT + 2):
        if it < NIT:
            stage1(it)
        if 0 <= it - 1 < NIT and int(os.environ.get("PHASEA_S2", "1")):
            stage2(it - 1)
        if 0 <= it - 2 < NIT and int(os.environ.get("PHASEA_S3", "1")):
            stage3(it - 2)
    if STOP < 2:
        P.emit()
        es.close()
        return nc

    QDEP = [("qkT", NSUB - 1), ("sigo", NSUB - 1)]
    outv = dr["mixA"].rearrange("p (a t) -> p a t", a=4)

    def sidx_of(d_, i):
        return i if d_ == 0 else NSUB - 1 - i

    def emit_ST(d_, i):
        s = sidx_of(d_, i)
        pS = ps[:, 512 * 5:512 * 5 + 128]
        qT = qkv[:, d_, 128 * s:128 * s + 128]
        kT = qkv[:, 2 + d_, 128 * s:128 * s + 128]
        mask = triF if d_ == 0 else triB
        P.add("pe", lambda e: e.matmul(pS, lhsT=kT, rhs=qT, start=True, stop=True),
              reads=[("qkT", s)] + QDEP, writes=[("ps", 5)])
        P.add("dve", lambda e: e.tensor_tensor(out=ptb[d_], in0=pS, in1=mask, op=ALU.mult),
              reads=["cf32"] + QDEP, preads=[("ps", 5)], writes=[("ptb", d_)])

    def scan_step(d_, i):
        s = sidx_of(d_, i)
        sprev = sidx_of(d_, i - 1)
        first, last = (i == 0), (i == NSUB - 1)
        pND = ps[:, 512 * 6:512 * 6 + VW]
        pDC = ps[:, 3584:3584 + VW]
        qT = qkv[:, d_, 128 * s:128 * s + 128]
        if first:
            emit_ST(d_, 0)
        while pending_y:
            emit_y_back(pending_y.pop(0))
        P.add("pe", lambda e: e.matmul(pND, lhsT=ptb[d_], rhs=vv[:, s, :], start=True, stop=first),
              reads=[("ptb", d_), ("v", s), "vones"], writes=[("ps", 6)])
        if not first:
            P.add("pe", lambda e: e.matmul(pND, lhsT=qT, rhs=Cbf[d_], start=False, stop=True),
                  reads=[("qkT", s), ("Cbf", d_)], writes=[("ps", 6)])
        P.add("pe", lambda e: e.matmul(pDC, lhsT=ktv[:, s, d_, :], rhs=vv[:, s, :], start=True, stop=True),
              reads=[("ktm", s, d_), ("v", s), "vones"], writes=[("ps", 7)])
        if not last:
            emit_ST(d_, i + 1)
        if first:
            P.add("dve", lambda e: e.tensor_copy(out=Dpre[d_], in_=pDC), reads=QDEP, preads=[("ps", 7)], writes=[("Dpre", d_)])
        else:
            P.add("dve", lambda e: e.scalar_tensor_tensor(out=Dpre[d_], in0=Dpre[d_], scalar=SCv[:, sprev, 2 + d_:3 + d_],
                                                          in1=pDC, op0=ALU.mult, op1=ALU.add),
                  reads=[("Dpre", d_), ("SC", sprev)], preads=[("ps", 7)], writes=[("Dpre", d_)])
        if not last:
            P.add("act", lambda e: e.activation(out=Cbf[d_], in_=Dpre[d_], func=AF.Copy, scale=SCv[:, s, 2 + d_:3 + d_]),
                  reads=[("Dpre", d_), ("SC", s)] + QDEP, writes=[("Cbf", d_)])
        r_ = rr[2 * d_ + (i % 2)]
        rk = ("rr", d_, i % 2)
        P.add("dve", lambda e: e.tensor_scalar(out=r_[:, 0:1], in0=pND[:, 256:257], scalar1=-1.0, scalar2=1.0,
                                               op0=ALU.mult, op1=ALU.max),
              reads=QDEP, preads=[("ps", 6)], writes=[rk])
        P.add("dve", lambda e: e.tensor_tensor(out=r_[:, 0:1], in0=r_[:, 0:1], in1=pND[:, 256:257], op=ALU.max),
              reads=[rk], preads=[("ps", 6)], writes=[rk])
        P.add("dve", lambda e: e.reciprocal(out=r_[:, 1:2], in_=r_[:, 0:1]), reads=[rk], writes=[rk])
        if s not in visited:
            visited.add(s)
            P.add("dve", lambda e: e.tensor_scalar(out=hv[:, s, :], in0=pND[:, 0:256], scalar1=r_[:, 1:2], scalar2=None, op0=ALU.mult),
                  reads=[rk] + QDEP, preads=[("ps", 6)], writes=[("h", s)])
        else:
            P.add("dve", lambda e: e.scalar_tensor_tensor(out=hv[:, s, :], in0=pND[:, 0:256], scalar=r_[:, 1:2], in1=hv[:, s, :],
                                                          op0=ALU.mult, op1=ALU.add),
                  reads=[rk, ("h", s)], preads=[("ps", 6)], writes=[("h", s)])
            par = s % 2
            P.add("act", lambda e: e.activation(out=yb[par], in_=hv[:, s, :], func=AF.Square, accum_out=msn[par][:, 0:1]),
                  reads=[("h", s)] + QDEP, writes=[("msn", par), ("yb", par)])
            P.add("act", lambda e: e.activation(out=msn[par][:, 1:2], in_=msn[par][:, 0:1], func=AF.Ln, bias=cst[:, 0:1], scale=1.0 / 256),
                  reads=[("msn", par), "cst"], writes=[("msn2", par)])
            P.add("act", lambda e: e.activation(out=msn[par][:, 1:2], in_=msn[par][:, 1:2], func=AF.Exp, scale=-0.5),
                  reads=[("msn2", par)], writes=[("msn2", par)])
            P.add("dve", lambda e: e.scalar_tensor_tensor(out=yb[par], in0=hv[:, s, :], scalar=msn[par][:, 1:2], in1=sov[:, s, :],
                                                          op0=ALU.mult, op1=ALU.mult),
                  reads=[("h", s), ("msn2", par), ("sigo", s), ("yb", par)], writes=[("yb", par)])
            pending_y.append(s)

    pending_y = []
    visited = set()

    def emit_y_back(s):
        par = s % 2
        pT2 = ps[:, 2048:2048 + 128].bitcast(BF16)
        for vc in range(2):
            P.add("pe", lambda e, vc=vc: e.transpose(pT2[:, 128 * vc:128 * vc + 128], yb[par][:, 128 * vc:128 * vc + 128], identb),
                  reads=[("yb", par), "identb"], writes=[("ps", 4)])
        ymv = yms[par].rearrange("p (a t) -> p a t", a=2)
        for vc in range(2):
            P.add("act", lambda e, vc=vc: e.activation(out=ymv[:, vc, :], in_=pT2[:, 128 * vc:128 * vc + 128], func=AF.Copy,
                                                       scale=mnw_sb[:, vc:vc + 1]),
                  reads=["mnw"] + QDEP, preads=[("ps", 4)], writes=[("yms", par, vc)])
        P.add("sp", lambda e: e.dma_start(out=outv[:, 2:4, 128 * s:128 * s + 128], in_=ymv),
              reads=[("yms", par, 0), ("yms", par, 1)], writes=[("outm", s)], dma=("outm", par))

    order = []
    for i in range(NSUB):
        order.append((0, i))
        order.append((1, i))
    sidx = 0
    NG = 32 // DFG
    total = 8 * NG

    def dft_load(j):
        buf = dbuf[j % NDB]
        P.add("sp", lambda e: e.dma_start(out=buf, in_=dr["dft"][j]), reads=QDEP, writes=[("dbuf", j % NDB)],
              dma=("dbuf", j % NDB))

    nxt = 0
    while nxt < min(NDB - 1, total):
        dft_load(nxt)
        nxt += 1
    for kt in range(8):
        for gi in range(NG):
            j = kt * NG + gi
            if nxt < total:
                dft_load(nxt)
                nxt += 1
            dv = dbuf[j % NDB].rearrange("p (r i k) -> p r i k", r=2, i=DFG)
            for i_ in range(DFG):
                nch = gi * DFG + i_
                for r_ in range(2):
                    for cch in range(2):
                        bank = 2 * r_ + cch
                        P.add("pe", lambda e, nch=nch, r_=r_, cch=cch, bank=bank, dv=dv, i_=i_: e.matmul(
                            ps[:, 512 * bank:512 * bank + 512], lhsT=Uv[:, nch, 128 * cch:128 * cch + 128],
                            rhs=dv[:, r_, i_, :], start=(nch == 0), stop=(nch == 31)),
                            reads=[("U", nch), ("dbuf", j % NDB)], writes=[("ps", bank)])
            if sidx < len(order):
                scan_step(*order[sidx])
                sidx += 1
        gt = GT[kt % 2]
        gtv = gt.rearrange("p (a k) -> p a k", a=4)
        for bank in range(4):
            if bank % 2 == 0:
                P.add("act", lambda e, bank=bank, gtv=gtv: e.activation(out=gtv[:, bank, :], in_=ps[:, 512 * bank:512 * bank + 512], func=AF.Copy),
                      reads=QDEP, preads=[("ps", bank)], writes=[("GT", kt % 2, bank)])
            else:
                P.add("dve", lambda e, bank=bank, gtv=gtv: e.tensor_copy(out=gtv[:, bank, :], in_=ps[:, 512 * bank:512 * bank + 512]),
                      reads=QDEP, preads=[("ps", bank)], writes=[("GT", kt % 2, bank)])
        yf = yfs[kt % 2]
        yfv = yf.rearrange("p (a k) -> p a k", a=2)
        for cp in range(2):
            n_ = 0
            for r_ in range(2):
                for cch in range(2):
                    P.add("pe", lambda e, cp=cp, r_=r_, cch=cch, gtv=gtv, n_=n_: e.matmul(
                        ps[:, 2048:2560], lhsT=cdv[:, cch, r_, 128 * cp:128 * cp + 128], rhs=gtv[:, 2 * r_ + cch, :],
                        start=(n_ == 0), stop=(n_ == 3)),
                        reads=["cdft", ("GT", kt % 2, 2 * r_ + cch)], writes=[("ps", 4)])
                    n_ += 1
            P.add("dve", lambda e, cp=cp, yfv=yfv: e.tensor_copy(out=yfv[:, cp, :], in_=ps[:, 2048:2560]),
                  reads=QDEP, preads=[("ps", 4)], writes=[("yfs", kt % 2, cp)])
        P.add("sp", lambda e, kt=kt, yfv=yfv: e.dma_start(out=outv[:, 0:2, 512 * kt:512 * kt + 512], in_=yfv),
              reads=[("yfs", kt % 2, 0), ("yfs", kt % 2, 1)], writes=[("outf", kt)], dma=("outf", kt % 2))
    while sidx < len(order):
        scan_step(*order[sidx])
        sidx += 1
    while pending_y:
        emit_y_back(pending_y.pop(0))
    P.add("sp", None, reads=[("outf", kt) for kt in range(8)] + [("outm", s) for s in range(NSUB)])
    P.emit()
    es.close()
    return nc


_CONST_CACHE = {}


def phase_a_consts():
    if "a" in _CONST_CACHE:
        return _CONST_CACHE["a"]
    bf = ml_dtypes.bfloat16
    t = np.arange(128)
    triF = (t[:, None] <= t[None, :]).astype(np.float32)
    triB = (t[:, None] >= t[None, :]).astype(np.float32)
    cf32 = np.ascontiguousarray(np.concatenate([triF, triB, np.ones((128, 128), np.float32)], 1))
    identb = np.eye(128, dtype=np.float32).astype(bf)
    c = np.arange(256)
    ang = 2 * np.pi * ((c[:, None] * c[None, :]) % 256) / 256.0
    cc = np.stack([np.cos(ang), -np.sin(ang)], 0) / 1024.0
    cd = cc.reshape(2, 2, 128, 256).transpose(2, 1, 0, 3)
    cdft = np.ascontiguousarray(cd).reshape(128, 1024).astype(bf)
    n = np.arange(S, dtype=np.int64)
    m = (n[:, None] * n[None, :]) % S
    ang = (2 * np.pi / S) * m
    NG = 32 // DFG
    out = np.empty((8 * NG, 128, 2, DFG, 512), bf)
    for r, fn in enumerate((np.cos, np.sin)):
        tab = fn(ang).astype(np.float32)
        tab = tab.reshape(NG, DFG, 128, 8, 512)
        out[:, :, r] = tab.transpose(3, 0, 2, 1, 4).reshape(8 * NG, 128, DFG, 512).astype(bf)
    dft = out.reshape(8 * NG, 128, 2 * DFG * 512)
    _CONST_CACHE["a"] = dict(cf32=cf32, identb=identb, cdft=cdft, dft=dft)
    return _CONST_CACHE["a"]


def phase_a_core_inputs(inp, shared, b, g):
    f = np.float32
    x = np.asarray(inp["x"], f)[b]
    xTa = np.ascontiguousarray(x.reshape(NTILE, 256, KC, 128).transpose(0, 3, 2, 1)).reshape(NTILE, 128, KC * 256)
    w_in = np.asarray(inp["w_in"], f)
    cols = np.concatenate([
        np.arange(256) + 256 * g,
        1024 + 128 * g + np.arange(128),
        1536 + 128 * g + np.arange(128),
        2048 + 256 * g + np.arange(256),
        3072 + 256 * g + np.arange(256),
        4096 + 4 * np.arange(4) + g,
    ])
    winA = np.ascontiguousarray(w_in[:, cols].reshape(KC, 128, NCOL))
    bgv = np.asarray(inp["b_gates"], f)[4 * np.arange(4) + g]
    bg = np.ascontiguousarray(np.broadcast_to(bgv[None, :], (128, 4)))
    mnw = np.ascontiguousarray(np.asarray(inp["mlstm_norm_w"], f)[256 * g:256 * g + 256].reshape(2, 128).T)
    cvec = np.ascontiguousarray(np.asarray(inp["c"], f)[b].reshape(KC, 128).T)
    d = dict(xTa=xTa, winA=winA, bg=bg, mnw=mnw, cvec=cvec,
             wada=shared["wada"], bada=shared["bada"], nw=shared["nw"])
    d.update(phase_a_consts())
    return d


def run_phase_a(inp, shared=None):
    nc = build_phase_a(bass.Bass("TRN2", target_bir_lowering=False))
    if shared is None:
        shared = prep_phase_b_weights(inp)
    in_maps = []
    for core in range(8):
        b, g = divmod(core, 4)
        in_maps.append(phase_a_core_inputs(inp, shared, b, g))
    res = run_bass_kernel_spmd(nc, in_maps, core_ids=list(range(8)))
    mix = np.zeros((NB, S, D), ml_dtypes.bfloat16)
    for core in range(8):
        b, g = divmod(core, 4)
        m = np.asarray(res.results[core]["mixA"]).reshape(128, 4, S)
        for a in range(4):
            f0 = (256 * g + 128 * a) if a < 2 else (1024 + 256 * g + 128 * (a - 2))
            mix[b, :, f0:f0 + 128] = m[:, a, :].T
    return mix


def kernel(**inputs):
    inp = {k: np.asarray(v) for k, v in inputs.items()}
    shared = prep_phase_b_weights(inp)
    mix = run_phase_a(inp, shared)
    return run_phase_b(inp, mix)
```
